# Optimizing a Trainium2 kernel written in Bass

```python
import jax, jax.numpy as jnp
from jax import lax
import numpy as np

D_MODEL = 4096
BATCH = 4
SEQ = 4096
DEPTH = 4

BLOCK = 128
RMS_EPS = 1e-6
HEAD_DIM_AB = 128
A_PAIRS = ((128, 1), (512, 4), (2048, 16))
A_HEADS_PER_GROUP = D_MODEL // 512
A_HEADS = A_HEADS_PER_GROUP * len(A_PAIRS)
B_HEADS = D_MODEL // 512
A_HW = A_HEADS * HEAD_DIM_AB
B_HW = B_HEADS * HEAD_DIM_AB
AB_IN = 3 * A_HW + 3 * B_HW
AB_OUT = (A_HEADS_PER_GROUP + B_HEADS) * HEAD_DIM_AB
C_HEAD_DIM = 64
C_Q_HEADS = D_MODEL // C_HEAD_DIM
C_KV_HEADS = 8
C_WINDOW = 128
C_IN = (C_Q_HEADS + 2 * C_KV_HEADS) * C_HEAD_DIM
C_OUT = C_Q_HEADS * C_HEAD_DIM
D_FF = 3 * D_MODEL // 2
N_EVEN = (DEPTH + 1) // 2
N_ODD = DEPTH // 2

kernel_name = "hybrid_dilated_stickbreak_swa_macaron"


def alibi_slopes(n):
    return jnp.asarray(2.0 ** (-8.0 * np.arange(1, n + 1) / n), dtype=jnp.float32)


def rms_norm(x, g):
    xf = x.astype(jnp.float32)
    y = xf * lax.rsqrt(jnp.mean(xf * xf, axis=-1, keepdims=True) + RMS_EPS)
    return (y * g.astype(jnp.float32)).astype(x.dtype)


def swiglu(x, w_in, w_out):
    gate, up = jnp.split(x @ w_in, 2, axis=-1)
    return (jax.nn.silu(gate) * up) @ w_out


def banded_attention(q, k, v, max_dist, slopes, dist_scale, sinks=None):
    N, L, H, hd = q.shape
    Hkv = k.shape[2]
    G = H // Hkv
    nb = -(-L // BLOCK)
    Lp = nb * BLOCK
    padw = ((0, 0), (0, Lp - L), (0, 0), (0, 0))
    qb = jnp.pad(q, padw).reshape(N, nb, BLOCK, Hkv, G, hd)
    kb = jnp.pad(k, padw).reshape(N, nb, BLOCK, Hkv, hd)
    vb = jnp.pad(v, padw).reshape(N, nb, BLOCK, Hkv, hd)
    kcat = jnp.concatenate([jnp.concatenate([jnp.zeros_like(kb[:, :1]), kb[:, :-1]], axis=1), kb], axis=2)
    vcat = jnp.concatenate([jnp.concatenate([jnp.zeros_like(vb[:, :1]), vb[:, :-1]], axis=1), vb], axis=2)
    s = jnp.einsum('nbqkgd,nbskd->nbkgqs', qb, kcat).astype(jnp.float32) * (hd ** -0.5)
    qi = jnp.arange(BLOCK)[:, None]
    si = jnp.arange(2 * BLOCK)[None, :]
    dist = qi - si + BLOCK
    key_pos = jnp.arange(nb)[:, None, None] * BLOCK - BLOCK + si[None]
    valid = (dist >= 0) & (dist <= max_dist) & (key_pos >= 0)
    bias = -(slopes.reshape(Hkv, G)[:, :, None, None] * (dist_scale * dist.astype(jnp.float32)))
    s = jnp.where(valid[None, :, None, None], s + bias, -jnp.inf)
    m = jnp.max(s, axis=-1)
    if sinks is not None:
        sk = sinks.astype(jnp.float32).reshape(Hkv, G)[:, :, None]
        m = jnp.maximum(m, sk)
    p = jnp.exp(s - m[..., None])
    denom = jnp.sum(p, axis=-1)
    if sinks is not None:
        denom = denom + jnp.exp(sk - m)
    o = jnp.einsum('nbkgqs,nbskd->nbqkgd', p, vcat.astype(jnp.float32))
    o = o / jnp.transpose(denom, (0, 1, 4, 2, 3))[..., None]
    lse = jnp.transpose(m + jnp.log(denom), (0, 1, 4, 2, 3)).reshape(N, Lp, H)
    return o.reshape(N, Lp, H, hd)[:, :L].astype(q.dtype), lse[:, :L]


def dilate(t, r):
    B, S = t.shape[:2]
    rest = t.shape[2:]
    return t.reshape((B, S // r, r) + rest).swapaxes(1, 2).reshape((B * r, S // r) + rest)


def undilate(t, r, B):
    L = t.shape[1]
    rest = t.shape[2:]
    return t.reshape((B, r, L) + rest).swapaxes(1, 2).reshape((B, r * L) + rest)


def stick_breaking(q, k, v):
    B, S, H, hd = q.shape
    outs = []
    for b in range(S // BLOCK):
        e = (b + 1) * BLOCK
        z = jnp.einsum('bqhd,bshd->bhqs', q[:, b * BLOCK:e], k[:, :e]).astype(jnp.float32) * (hd ** -0.5)
        causal = jnp.arange(e)[None, :] < (b * BLOCK + jnp.arange(BLOCK))[:, None]
        log_keep = jnp.where(causal, jax.nn.log_sigmoid(-z), 0.0)
        after = lax.cumsum(log_keep, axis=3, reverse=True) - log_keep
        a = jnp.where(causal, jnp.exp(jax.nn.log_sigmoid(z) + after), 0.0)
        outs.append(jnp.einsum('bhqs,bshd->bqhd', a, v[:, :e].astype(jnp.float32)))
    return jnp.concatenate(outs, axis=1).astype(q.dtype)


def mixer_ab(h, w_in, w_out):
    B, S, _ = h.shape
    cuts = [A_HW, 2 * A_HW, 3 * A_HW, 3 * A_HW + B_HW, 3 * A_HW + 2 * B_HW]
    qa, ka, va, qb, kb, vb = jnp.split(h @ w_in, cuts, axis=-1)
    qa, ka, va = (t.reshape(B, S, A_HEADS, HEAD_DIM_AB) for t in (qa, ka, va))
    qb, kb, vb = (t.reshape(B, S, B_HEADS, HEAD_DIM_AB) for t in (qb, kb, vb))
    slopes = alibi_slopes(A_HEADS)
    outs, lses = [], []
    for gi, (window, r) in enumerate(A_PAIRS):
        sl = slice(gi * A_HEADS_PER_GROUP, (gi + 1) * A_HEADS_PER_GROUP)
        o, l = banded_attention(dilate(qa[:, :, sl], r), dilate(ka[:, :, sl], r), dilate(va[:, :, sl], r),
                                window // r, slopes[sl], r)
        outs.append(undilate(o, r, B))
        lses.append(undilate(l, r, B))
    wts = jax.nn.softmax(jnp.stack(lses), axis=0)
    out_a = jnp.sum(wts[..., None] * jnp.stack(outs).astype(jnp.float32), axis=0).astype(h.dtype)
    out_b = stick_breaking(qb, kb, vb)
    cat = jnp.concatenate([out_a.reshape(B, S, -1), out_b.reshape(B, S, -1)], axis=-1)
    return cat @ w_out


def mixer_c(h, w_in, w_out, sinks):
    B, S, _ = h.shape
    kvw = C_KV_HEADS * C_HEAD_DIM
    q, k, v = jnp.split(h @ w_in, [C_OUT, C_OUT + kvw], axis=-1)
    q = q.reshape(B, S, C_Q_HEADS, C_HEAD_DIM)
    k = k.reshape(B, S, C_KV_HEADS, C_HEAD_DIM)
    v = v.reshape(B, S, C_KV_HEADS, C_HEAD_DIM)
    o, _ = banded_attention(q, k, v, C_WINDOW - 1, alibi_slopes(C_Q_HEADS), 1.0, sinks)
    return o.reshape(B, S, C_OUT) @ w_out


def setup_inputs(seed: int = 0) -> dict:
    key = jax.random.key(seed)
    ks = jax.random.split(key, 16)
    f32 = jnp.float32

    def w(k, shape, fan_in):
        return jax.random.normal(k, shape, f32) * (fan_in ** -0.5)

    def gain(k, shape):
        return 1.0 + 0.01 * jax.random.normal(k, shape, f32)

    return {
        "x": jax.random.normal(ks[0], (BATCH, SEQ, D_MODEL), f32),
        "ffn1_norm": gain(ks[1], (DEPTH, D_MODEL)),
        "ffn1_w_in": w(ks[2], (DEPTH, D_MODEL, 2 * D_FF), D_MODEL),
        "ffn1_w_out": w(ks[3], (DEPTH, D_FF, D_MODEL), D_FF),
        "mix_norm": gain(ks[4], (DEPTH, D_MODEL)),
        "ab_w_in": w(ks[5], (N_EVEN, D_MODEL, AB_IN), D_MODEL),
        "ab_w_out": w(ks[6], (N_EVEN, AB_OUT, D_MODEL), AB_OUT),
        "c_w_in": w(ks[7], (N_ODD, D_MODEL, C_IN), D_MODEL),
        "c_w_out": w(ks[8], (N_ODD, C_OUT, D_MODEL), C_OUT),
        "c_sinks": 0.5 * jax.random.normal(ks[9], (N_ODD, C_Q_HEADS), f32),
        "ffn2_norm": gain(ks[10], (DEPTH, D_MODEL)),
        "ffn2_w_in": w(ks[11], (DEPTH, D_MODEL, 2 * D_FF), D_MODEL),
        "ffn2_w_out": w(ks[12], (DEPTH, D_FF, D_MODEL), D_FF),
        "final_norm": gain(ks[13], (D_MODEL,)),
    }


def reference(x, ffn1_norm, ffn1_w_in, ffn1_w_out, mix_norm, ab_w_in, ab_w_out, c_w_in, c_w_out,
              c_sinks, ffn2_norm, ffn2_w_in, ffn2_w_out, final_norm):
    for l in range(DEPTH):
        x = x + 0.5 * swiglu(rms_norm(x, ffn1_norm[l]), ffn1_w_in[l], ffn1_w_out[l])
        h = rms_norm(x, mix_norm[l])
        if l % 2 == 0:
            x = x + mixer_ab(h, ab_w_in[l // 2], ab_w_out[l // 2])
        else:
            x = x + mixer_c(h, c_w_in[l // 2], c_w_out[l // 2], c_sinks[l // 2])
        x = x + 0.5 * swiglu(rms_norm(x, ffn2_norm[l]), ffn2_w_in[l], ffn2_w_out[l])
    return rms_norm(x, final_norm)
```

```python
import numpy as np
import ml_dtypes
import concourse.bass as bass
import concourse.mybir as mybir
from concourse.bass_utils import run_bass_kernel_spmd

F32 = mybir.dt.float32
BF16 = mybir.dt.bfloat16
AF = mybir.ActivationFunctionType
ALU = mybir.AluOpType
AX = mybir.AxisListType
BIG = 1.0e6
RMS_EPS = 1e-6


class Op:
    __slots__ = ("eng", "fn", "deps", "needs_inc", "token", "is_dma", "ndma")

    def __init__(self, eng, fn, is_dma=False, ndma=0):
        self.eng = eng
        self.fn = fn
        self.deps = []
        self.needs_inc = False
        self.token = None
        self.is_dma = is_dma
        self.ndma = ndma


class Sched:
    ENGS = ("pe", "dve", "act", "pool", "sp")
    SEM_LIMIT = 20000
    NDMA_SEMS = 24
    SAME_ENG_SYNC = True

    def __init__(self, nc):
        self.nc = nc
        self.ops = {e: [] for e in self.ENGS}
        self.last_writer = {}
        self.readers = {}
        self.dma_slot_last = [None] * self.NDMA_SEMS
        self.dma_slot_total = [0] * self.NDMA_SEMS
        self.dma_i = 0
        self.pending_dma = []
        self.dma_sems = [nc.alloc_semaphore(f"dq{i}") for i in range(self.NDMA_SEMS)]
        self.eng_sems = {e: [] for e in self.ENGS}
        self.nops = 0

    def add(self, eng, fn, reads=(), writes=(), ndma=0):
        op = Op(eng, fn, is_dma=ndma > 0, ndma=ndma)
        deps = set()
        for r in reads:
            lw = self.last_writer.get(r)
            if lw is not None:
                deps.add(lw)
        for w in writes:
            lw = self.last_writer.get(w)
            if lw is not None:
                deps.add(lw)
            for rd in self.readers.get(w, ()):
                deps.add(rd)
        for r in reads:
            self.readers.setdefault(r, []).append(op)
        for w in writes:
            self.last_writer[w] = op
            self.readers[w] = []
        if ndma:
            slot = self.dma_i % self.NDMA_SEMS
            self.dma_i += 1
            prev = self.dma_slot_last[slot]
            if prev is not None:
                deps.add(prev)
            self.dma_slot_total[slot] += 16 * ndma
            op.token = (self.dma_sems[slot], self.dma_slot_total[slot])
            self.dma_slot_last[slot] = op
            op.needs_inc = True
            self.pending_dma.append(op)
        for d in deps:
            if d is op:
                continue
            if d.is_dma or d.eng != eng or (self.SAME_ENG_SYNC and eng != "pe"):
                op.deps.append(d)
                d.needs_inc = True
        self.ops[eng].append(op)
        self.nops += 1
        return op

    def barrier(self):
        lasts = []
        for e in self.ENGS:
            for op in reversed(self.ops[e]):
                if not op.is_dma and op.fn is not None:
                    op.needs_inc = True
                    lasts.append(op)
                    break
        pend = list(self.pending_dma)
        for e in self.ENGS:
            b = Op(e, None)
            b.deps = [o for o in lasts if o.eng != e] + pend
            self.ops[e].append(b)
        self.pending_dma = []
        self.last_writer = {}
        self.readers = {}
        self.dma_slot_last = [None] * self.NDMA_SEMS

    def emit(self):
        nc = self.nc
        for e in self.ENGS:
            cnt = 0
            k = 0
            for op in self.ops[e]:
                if op.is_dma or not op.needs_inc:
                    continue
                if cnt >= self.SEM_LIMIT:
                    k += 1
                    cnt = 0
                while len(self.eng_sems[e]) <= k:
                    self.eng_sems[e].append(nc.alloc_semaphore(f"s_{e}{len(self.eng_sems[e])}"))
                cnt += 1
                op.token = (self.eng_sems[e][k], cnt, k)
        engmap = {"pe": "tensor", "dve": "vector", "act": "scalar", "pool": "gpsimd", "sp": "sync"}
        with nc.Block() as block:
            for e in self.ENGS:
                def body(eng, e=e):
                    waited = {}
                    for op in self.ops[e]:
                        need = {}
                        for d in op.deps:
                            t = d.token
                            sem, val = t[0], t[1]
                            key = id(sem)
                            if key not in need or need[key][1] < val:
                                need[key] = (sem, val)
                        for key, (sem, val) in need.items():
                            if waited.get(key, 0) < val:
                                eng.wait_ge(sem, val)
                                waited[key] = val
                        if op.fn is None:
                            continue
                        res = op.fn(eng)
                        if op.needs_inc:
                            if op.is_dma:
                                assert isinstance(res, (list, tuple)) and len(res) == op.ndma
                                for ins in res:
                                    ins.then_inc(op.token[0], 16)
                            else:
                                ins = res[-1] if isinstance(res, (list, tuple)) else res
                                ins.then_inc(op.token[0], 1)
                getattr(block, engmap[e])(body)


def R(t, *idx):
    return (id(t),) + tuple(idx)


class Cfg:
    def __init__(self, D=4096, S=4096, B=4, depth=4):
        self.D = D
        self.S = S
        self.B = B
        self.depth = depth
        self.NT = S // 2
        self.HG = D // 512
        self.FF = 3 * D // 2
        self.A_HW = 3 * self.HG * 128
        self.B_HW = self.HG * 128
        self.AB_IN = 3 * self.A_HW + 3 * self.B_HW
        self.AB_OUT = 2 * self.HG * 128
        self.CQ = D // 64
        self.CKV = 8
        self.C_IN = (self.CQ + 16) * 64
        self.C_OUT = D
        self.KC = D // 128
        self.T = 512
        self.NTT = self.NT // self.T


A_PAIRS = ((128, 1), (512, 4), (2048, 16))


def alibi(n):
    return [float(2.0 ** (-8.0 * (i + 1) / n)) for i in range(n)]


class Prog:
    def __init__(self, cfg):
        self.cfg = cfg
        self.nc = bass.Bass("TRN2", target_bir_lowering=False)
        self.s = Sched(self.nc)
        self.dram = {}
        self.stack = None
        self.evac_flip = 0

    def din(self, name, shape, dt=F32):
        t = self.nc.dram_tensor(name, list(shape), dt, kind="ExternalInput").ap()
        self.dram[name] = t
        return t

    def dout(self, name, shape, dt=F32):
        t = self.nc.dram_tensor(name, list(shape), dt, kind="ExternalOutput").ap()
        self.dram[name] = t
        return t

    def dint(self, name, shape, dt=F32):
        t = self.nc.dram_tensor(name, list(shape), dt, kind="Internal").ap()
        self.dram[name] = t
        return t

    def sb(self, name, shape, dt):
        self.nsb = getattr(self, "nsb", 0) + 1
        return self.stack.enter_context(self.nc.sbuf_tensor(f"sb{self.nsb}_{name}", list(shape), dt))

    def ps(self, name):
        self.nsb = getattr(self, "nsb", 0) + 1
        return self.stack.enter_context(self.nc.psum_tensor(f"ps{self.nsb}_{name}", [128, 512], F32))

    def dma(self, q, out, in_, reads=(), writes=()):
        self.s.add(q, lambda e, out=out, in_=in_: [e.dma_start(out=out, in_=in_)], reads=reads, writes=writes, ndma=1)

    def dmas(self, q, pairs, reads=(), writes=()):
        self.s.add(q, lambda e, pairs=pairs: [e.dma_start(out=o, in_=i) for (o, i) in pairs], reads=reads,
                   writes=writes, ndma=len(pairs))

    def op(self, eng, fn, reads=(), writes=()):
        self.s.add(eng, fn, reads=reads, writes=writes)

    def evac_copy(self, out, in_, reads, writes):
        self.evac_flip ^= 1
        if self.evac_flip:
            self.op("act", lambda e: e.copy(out=out, in_=in_), reads, writes)
        else:
            self.op("dve", lambda e: e.tensor_copy(out=out, in_=in_), reads, writes)


class Lin:
    NU = 6

    def __init__(self, P):
        cfg = P.cfg
        self.P = P
        T = cfg.T
        KCmax = max(cfg.KC, cfg.FF // 128)
        self.hT = P.sb("hT", [128, cfg.KC, T], BF16)
        self.gT = P.sb("gT", [128, cfg.FF // 128, T], BF16)
        self.units = [P.sb(f"wu{i}", [128, 4096], BF16) for i in range(self.NU)]
        self.ui = 0
        self.xt = [P.sb(f"xt{i}", [128, cfg.D], F32) for i in range(2)]
        self.xi = 0
        self.xn = P.sb("xn", [128, cfg.D], BF16)
        self.gbc = P.sb("gbc", [128, cfg.D], F32)
        self.sg = [P.sb(f"sg{i}", [128, 512], F32) for i in range(2)]
        self.sgi = 0
        self.st = [P.sb(f"st{i}", [128, 512], BF16) for i in range(3)]
        self.sti = 0
        self.xr = [P.sb(f"xr{i}", [128, 512], F32) for i in range(4)]
        self.xri = 0
        self.ss = P.sb("ss", [128, 4], F32)
        self.ident = P.sb("ident", [128, 128], BF16)
        self.banks = [P.ps(f"pb{i}") for i in range(8)]
        self.bi = 0
        P.dma("sp", self.ident[:], P.dram["ident"][:, :], writes=[R(self.ident)])

    def bank(self):
        b = self.banks[self.bi % 8]
        self.bi += 1
        return b

    def unit(self):
        u = self.units[self.ui % self.NU]
        self.ui += 1
        return u

    def load_gain(self, g_ap):
        P = self.P
        P.dma("sp", self.gbc[:], g_ap.partition_broadcast(128), writes=[R(self.gbc)])

    def load_w_units(self, w, k0, nk, c0, ncols):
        P = self.P
        u = self.unit()
        uv = u[:, 0:nk * ncols].rearrange("p (k n) -> p k n", n=ncols)
        wv = w[k0 * 128:(k0 + nk) * 128, c0:c0 + ncols].rearrange("(k p) n -> p k n", p=128)
        pairs = []
        step = max(1, 8 * 256 // ncols) if ncols <= 256 else 8
        for a in range(0, nk, step):
            b = min(nk, a + step)
            pairs.append((uv[:, a:b, :], wv[:, a:b, :]))
        P.dmas("pool", pairs, writes=[R(u)])
        return u, uv

    def norm_to_hT(self, src_rows_fn, ntok_sub, x_res_fn, keep=None):
        P, cfg = self.P, self.P.cfg
        D = cfg.D
        for i in range(ntok_sub):
            xt = self.xt[self.xi % 2]
            self.xi += 1
            P.dma("sp", xt[:], src_rows_fn(i), reads=x_res_fn(i), writes=[R(xt)])
            ss = self.ss
            P.op("dve", lambda e: e.memset(ss[:, 0:1], 0.0), writes=[R(ss)])
            xn = self.xn
            P.op("act", lambda e, xt=xt: e.activation(out=xn[:], in_=xt[:], func=AF.Square, accum_out=ss[:, 0:1]),
                 reads=[R(xt), R(ss)], writes=[R(xn), R(ss)])
            P.op("dve", lambda e: e.tensor_scalar(out=ss[:, 1:2], in0=ss[:, 0:1], scalar1=1.0 / D, scalar2=RMS_EPS,
                                                  op0=ALU.mult, op1=ALU.add), reads=[R(ss)], writes=[R(ss)])
            P.op("act", lambda e: e.activation(out=ss[:, 3:4], in_=ss[:, 1:2], func=AF.Sqrt), reads=[R(ss)], writes=[R(ss)])
            P.op("dve", lambda e: e.reciprocal(out=ss[:, 2:3], in_=ss[:, 3:4]), reads=[R(ss)], writes=[R(ss)])
            P.op("dve", lambda e, xt=xt: e.scalar_tensor_tensor(out=xn[:], in0=xt[:], scalar=ss[:, 2:3], in1=self.gbc[:],
                                                                op0=ALU.mult, op1=ALU.mult),
                 reads=[R(xt), R(ss), R(self.gbc)], writes=[R(xn)])
            self.transpose_into_hT(xn, cfg.KC, i)

    def transpose_into_hT(self, src_bf, kc_n, i):
        P = self.P
        for k0 in range(0, kc_n, 8):
            nk = min(8, kc_n - k0)
            bank = self.bank()
            bv = bank[:].bitcast(BF16)[:, 0:nk * 128].rearrange("p (k t) -> p k t", t=128)
            for k in range(nk):
                P.op("pe", lambda e, k=k, k0=k0, bv=bv: e.transpose(out=bv[:, k, :], in_=src_bf[:, (k0 + k) * 128:(k0 + k + 1) * 128],
                                                                    identity=self.ident[:]),
                     reads=[R(src_bf), R(self.ident)], writes=[R(bank)])
            P.evac_copy(self.hT[:, k0:k0 + nk, i * 128:(i + 1) * 128], bv[:, :, :], reads=[R(bank)], writes=[R(self.hT, i)])

    def mm_A(self, w, c0, ncols, KC, actT, act_res, T, consume):
        P = self.P
        nm = ncols // 128
        banks = [self.bank() for _ in range(nm)]
        for k0 in range(0, KC, 16):
            nk = min(16, KC - k0)
            u, uv = self.load_w_units(w, k0, nk, c0, ncols)
            for m in range(nm):
                for k in range(nk):
                    kg = k0 + k
                    P.op("pe", lambda e, m=m, k=k, kg=kg, uv=uv, bank=banks[m]: e.matmul(
                        bank[:, 0:T], lhsT=uv[:, k, m * 128:(m + 1) * 128], rhs=actT[:, kg, 0:T],
                        start=(kg == 0), stop=(kg == KC - 1)),
                        reads=[R(u)] + act_res, writes=[R(banks[m])])
        for m in range(nm):
            consume(m, banks[m])

    def mm_B(self, w, c0, ncols, KC, actT, act_res_fn, nsub, consume):
        P = self.P
        banks = [self.bank() for _ in range(nsub)]
        for k0 in range(0, KC, 8):
            nk = min(8, KC - k0)
            u, uv = self.load_w_units(w, k0, nk, c0, ncols)
            for i in range(nsub):
                for k in range(nk):
                    kg = k0 + k
                    P.op("pe", lambda e, i=i, k=k, kg=kg, uv=uv, bank=banks[i]: e.matmul(
                        bank[:, 0:ncols], lhsT=actT[:, kg, i * 128:(i + 1) * 128], rhs=uv[:, k, :],
                        start=(kg == 0), stop=(kg == KC - 1)),
                        reads=[R(u)] + act_res_fn(i), writes=[R(banks[i])])
        for i in range(nsub):
            consume(i, banks[i])


def ffn_phase(P, L, x, gain, w_in, w_out):
    cfg = P.cfg
    T, D, FF = cfg.T, cfg.D, cfg.FF
    nsub = T // 128
    L.load_gain(gain)
    for tt in range(cfg.NTT):
        t0 = tt * T
        L.norm_to_hT(lambda i: x[t0 + i * 128:t0 + (i + 1) * 128, :], nsub, lambda i: xkeys(cfg, tt, i))
        hres = [R(L.hT, i) for i in range(nsub)]
        for j in range(FF // 256):
            gb = {}

            def cons_g(m, bank, gb=gb):
                gb[m] = bank

            L.mm_A(w_in, j * 256, 256, cfg.KC, L.hT, hres, T, cons_g)

            def cons_u(m, bank, gb=gb, j=j):
                sg = L.sg[L.sgi % 2]
                L.sgi += 1
                P.op("act", lambda e, sg=sg, g=gb[m]: e.activation(out=sg[:, 0:T], in_=g[:, 0:T], func=AF.Silu),
                     reads=[R(gb[m])], writes=[R(sg)])
                P.op("dve", lambda e, sg=sg, bank=bank, c=j * 2 + m: e.tensor_tensor(out=L.gT[:, c, 0:T], in0=sg[:, 0:T], in1=bank[:, 0:T],
                                                                                op=ALU.mult),
                     reads=[R(sg), R(bank)], writes=[R(L.gT, j * 2 + m)])

            L.mm_A(w_in, FF + j * 256, 256, cfg.KC, L.hT, hres, T, cons_u)
        gres = [R(L.gT, c) for c in range(FF // 128)]
        resid_out(P, L, x, w_out, FF // 128, L.gT, gres, t0, tt, 0.5)


def xkeys(cfg, tt, i):
    return [("x", tt, i, s) for s in range((cfg.D + 511) // 512)]


def resid_out(P, L, x, w_out, KC, actT, act_res, t0, tt, scale):
    cfg = P.cfg
    D = cfg.D
    nsub = cfg.T // 128
    for n0 in range(0, D, 512):
        ncols = min(512, D - n0)

        def cons_o(i, bank, n0=n0, ncols=ncols):
            xr = L.xr[L.xri % len(L.xr)]
            L.xri += 1
            key = ("x", tt, i, n0 // 512)
            P.dma("sp", xr[:, 0:ncols], x[t0 + i * 128:t0 + (i + 1) * 128, n0:n0 + ncols],
                  reads=[key], writes=[R(xr)])
            P.op("dve", lambda e, xr=xr, bank=bank: e.scalar_tensor_tensor(out=xr[:, 0:ncols], in0=bank[:, 0:ncols], scalar=scale,
                                                                         in1=xr[:, 0:ncols], op0=ALU.mult, op1=ALU.add),
                 reads=[R(bank), R(xr)], writes=[R(xr)])
            P.dma("sp", x[t0 + i * 128:t0 + (i + 1) * 128, n0:n0 + ncols], xr[:, 0:ncols],
                  reads=[R(xr)], writes=[key])

        L.mm_B(w_out, n0, ncols, KC, actT, lambda i: act_res, nsub, cons_o)


def final_norm_phase(P, L, x, gain, out):
    cfg = P.cfg
    D = cfg.D
    L.load_gain(gain)
    for tt in range(cfg.NTT):
        for i in range(cfg.T // 128):
            r0 = tt * cfg.T + i * 128
            xt = L.xt[L.xi % 2]
            L.xi += 1
            P.dma("sp", xt[:], x[r0:r0 + 128, :], reads=xkeys(cfg, tt, i), writes=[R(xt)])
            ss = L.ss
            P.op("dve", lambda e: e.memset(ss[:, 0:1], 0.0), writes=[R(ss)])
            P.op("act", lambda e, xt=xt: e.activation(out=L.xn[:], in_=xt[:], func=AF.Square, accum_out=ss[:, 0:1]),
                 reads=[R(xt), R(ss)], writes=[R(L.xn), R(ss)])
            P.op("dve", lambda e: e.tensor_scalar(out=ss[:, 1:2], in0=ss[:, 0:1], scalar1=1.0 / D, scalar2=RMS_EPS,
                                                  op0=ALU.mult, op1=ALU.add), reads=[R(ss)], writes=[R(ss)])
            P.op("act", lambda e: e.activation(out=ss[:, 3:4], in_=ss[:, 1:2], func=AF.Sqrt), reads=[R(ss)], writes=[R(ss)])
            P.op("dve", lambda e: e.reciprocal(out=ss[:, 2:3], in_=ss[:, 3:4]), reads=[R(ss)], writes=[R(ss)])
            P.op("dve", lambda e, xt=xt: e.scalar_tensor_tensor(out=xt[:], in0=xt[:], scalar=ss[:, 2:3], in1=L.gbc[:],
                                                                op0=ALU.mult, op1=ALU.mult),
                 reads=[R(xt), R(ss), R(L.gbc)], writes=[R(xt)])
            P.dma("sp", out[r0:r0 + 128, :], xt[:], reads=[R(xt)], writes=[("out", tt, i)])


def evac_scaled(P, out, in_, scale, reads, writes):
    P.evac_flip ^= 1
    if scale == 1.0:
        if P.evac_flip:
            P.op("act", lambda e: e.copy(out=out, in_=in_), reads, writes)
        else:
            P.op("dve", lambda e: e.tensor_copy(out=out, in_=in_), reads, writes)
    else:
        if P.evac_flip:
            P.op("act", lambda e: e.mul(out=out, in_=in_, mul=scale), reads, writes)
        else:
            P.op("dve", lambda e: e.tensor_scalar(out=out, in0=in_, scalar1=scale, scalar2=None, op0=ALU.mult), reads, writes)


def qkv_phase(P, L, x, gain, w, fm_chunks, v_slabs):
    cfg = P.cfg
    T = cfg.T
    nsub = T // 128
    L.load_gain(gain)
    for tt in range(cfg.NTT):
        t0 = tt * T
        L.norm_to_hT(lambda i: x[t0 + i * 128:t0 + (i + 1) * 128, :], nsub, lambda i: xkeys(cfg, tt, i))
        hres = [R(L.hT, i) for i in range(nsub)]
        ci = 0
        while ci < len(fm_chunks):
            grp = [fm_chunks[ci]]
            if ci + 1 < len(fm_chunks) and fm_chunks[ci + 1][0] == fm_chunks[ci][0] + 128:
                grp.append(fm_chunks[ci + 1])
            ci += len(grp)

            def cons(m, bank, grp=grp):
                c0, dst, r, scale = grp[m]
                st = L.st[L.sti % len(L.st)]
                L.sti += 1
                ov = st[:, 0:T].rearrange("p (r n) -> p r n", r=r)
                iv = bank[:, 0:T].rearrange("p (n r) -> p r n", r=r)
                evac_scaled(P, ov, iv, scale, [R(bank)], [R(st)])
                dv = dst.rearrange("p (r n) -> p r n", r=r)[:, :, t0 // r:(t0 + T) // r]
                P.dma("sp", dv, ov, reads=[R(st)], writes=[("fm", id(dst), tt)])

            L.mm_A(w, grp[0][0], 128 * len(grp), cfg.KC, L.hT, hres, T, cons)
        for (c0, ncols, dst) in v_slabs:
            for n0 in range(0, ncols, 512):
                nc_ = min(512, ncols - n0)

                def consv(i, bank, n0=n0, nc_=nc_, dst=dst):
                    st = L.st[L.sti % len(L.st)]
                    L.sti += 1
                    evac_scaled(P, st[:, 0:nc_], bank[:, 0:nc_], 1.0, [R(bank)], [R(st)])
                    P.dma("sp", dst[t0 + i * 128:t0 + (i + 1) * 128, n0:n0 + nc_], st[:, 0:nc_], reads=[R(st)],
                          writes=[("tm", id(dst), tt, i, n0)])

                L.mm_B(w, c0 + n0, nc_, cfg.KC, L.hT, lambda i: hres, nsub, consv)


def outproj_phase(P, L, x, cat, C, w_out):
    cfg = P.cfg
    T = cfg.T
    nsub = T // 128
    for tt in range(cfg.NTT):
        t0 = tt * T
        for i in range(nsub):
            ct = L.xn
            P.dma("sp", ct[:, 0:C], cat[t0 + i * 128:t0 + (i + 1) * 128, :], writes=[R(ct)])
            L.transpose_into_hT(ct, C // 128, i)
        hres = [R(L.hT, i) for i in range(nsub)]
        resid_out(P, L, x, w_out, C // 128, L.hT, hres, t0, tt, 1.0)


class Att:
    def __init__(self, P, dk, hd, HL, shared_kv, with_lse, with_sink):
        self.P = P
        cfg = P.cfg
        self.dk, self.hd, self.HL = dk, hd, HL
        self.HU = min(4, HL)
        HU = self.HU
        NB = 3
        nkv = 1 if shared_kv else HL
        self.qb = [P.sb(f"aq{i}", [dk, HL, 128], BF16) for i in range(NB)]
        self.kb = [P.sb(f"ak{i}", [dk, nkv, 256], BF16) for i in range(NB)]
        self.vb = [P.sb(f"av{i}", [128, 2, nkv * hd], BF16) for i in range(NB)]
        self.li = 0
        self.sbuf_s = [P.sb(f"as{i}", [128, HU, 256], F32) for i in range(2)]
        self.p = [P.sb(f"ap{i}", [128, HU, 256], BF16) for i in range(2)]
        self.pT = [P.sb(f"apT{i}", [128, 2 * HU, 128], BF16) for i in range(2)]
        self.osb = [P.sb(f"ao{i}", [128, HU, hd], F32 if with_lse else BF16) for i in range(3)]
        self.sm = [P.sb(f"asm{i}", [128, 8, HU], F32) for i in range(3)]
        self.ui = 0
        self.ident = P.sb("aident", [128, 128], BF16)
        self.dist = P.sb("adist", [128, 2, 256], F32)
        self.banks = [P.ps(f"ab{i}") for i in range(8)]
        P.dma("sp", self.ident[:], P.dram["ident"][:, :], writes=[R(self.ident)])
        if with_sink:
            self.sink = P.sb("asink", [128, cfg.CQ], F32)


def banded_unit(P, A, qb, kb, vb, heads, kv_of, slopes, first, sink_cols, o_dst, lse_dst, res_in):
    HU = len(heads)
    hd = A.hd
    u = A.ui
    A.ui += 1
    b2 = u % 2
    sc = [A.banks[b2 * 2], A.banks[b2 * 2 + 1]]
    pTp = A.banks[4 + b2]
    ops_ = A.banks[6 + b2]
    sbs = A.sbuf_s[b2]
    p = A.p[b2]
    pT = A.pT[b2]
    osb = A.osb[u % 3]
    sm = A.sm[u % 3]
    dist = A.dist[:, 1 if first else 0, :]
    for j, h in enumerate(heads):
        P.op("pe", lambda e, j=j, h=h: e.matmul(sc[j // 2][:, (j % 2) * 256:(j % 2) * 256 + 256], lhsT=qb[:, h, :],
                                               rhs=kb[:, kv_of(h), :], start=True, stop=True),
             reads=res_in, writes=[R(sc[j // 2])])
    for j, h in enumerate(heads):
        P.op("dve", lambda e, j=j, h=h: e.scalar_tensor_tensor(out=sbs[:, j, :], in0=dist, scalar=-slopes[h],
                                                               in1=sc[j // 2][:, (j % 2) * 256:(j % 2) * 256 + 256],
                                                               op0=ALU.mult, op1=ALU.add),
             reads=[R(sc[j // 2]), R(A.dist)], writes=[R(sbs)])
    P.op("dve", lambda e: e.tensor_reduce(out=sm[:, 0, 0:HU], in_=sbs[:, 0:HU, :], axis=AX.X, op=ALU.max),
         reads=[R(sbs)], writes=[R(sm)])
    if sink_cols is not None:
        P.op("dve", lambda e: e.tensor_tensor(out=sm[:, 0, 0:HU], in0=sm[:, 0, 0:HU], in1=sink_cols, op=ALU.max),
             reads=[R(sm), R(A.sink)], writes=[R(sm)])
    P.op("dve", lambda e: e.tensor_scalar(out=sm[:, 1, 0:HU], in0=sm[:, 0, 0:HU], scalar1=-1.0, scalar2=None, op0=ALU.mult),
         reads=[R(sm)], writes=[R(sm)])
    P.op("dve", lambda e: e.memset(sm[:, 2, 0:HU], 0.0), reads=[], writes=[R(sm)])
    for j in range(HU):
        P.op("act", lambda e, j=j: e.activation(out=p[:, j, :], in_=sbs[:, j, :], func=AF.Exp, bias=sm[:, 1, j:j + 1],
                                                scale=1.0, accum_out=sm[:, 2, j:j + 1]),
             reads=[R(sbs), R(sm)], writes=[R(p), R(sm)])
    if sink_cols is not None:
        P.op("dve", lambda e: e.tensor_tensor(out=sm[:, 4, 0:HU], in0=sink_cols, in1=sm[:, 1, 0:HU], op=ALU.add),
             reads=[R(sm), R(A.sink)], writes=[R(sm)])
        P.op("act", lambda e: e.activation(out=sm[:, 5, 0:HU], in_=sm[:, 4, 0:HU], func=AF.Exp), reads=[R(sm)], writes=[R(sm)])
        P.op("dve", lambda e: e.tensor_tensor(out=sm[:, 2, 0:HU], in0=sm[:, 2, 0:HU], in1=sm[:, 5, 0:HU], op=ALU.add),
             reads=[R(sm)], writes=[R(sm)])
    pTv = pTp[:].bitcast(BF16)[:, 0:2 * HU * 128].rearrange("p (k t) -> p k t", t=128)
    for j in range(HU):
        for kh in range(2):
            P.op("pe", lambda e, j=j, kh=kh: e.transpose(out=pTv[:, j * 2 + kh, :], in_=p[:, j, kh * 128:(kh + 1) * 128],
                                                         identity=A.ident[:]),
                 reads=[R(p), R(A.ident)], writes=[R(pTp)])
    P.evac_copy(pT[:, 0:2 * HU, :], pTv[:, :, :], reads=[R(pTp)], writes=[R(pT)])
    for j, h in enumerate(heads):
        kv = kv_of(h)
        for kh in range(2):
            P.op("pe", lambda e, j=j, kh=kh, kv=kv: e.matmul(ops_[:, j * hd:(j + 1) * hd], lhsT=pT[:, j * 2 + kh, :],
                                                            rhs=vb[:, kh, kv * hd:(kv + 1) * hd], start=(kh == 0), stop=(kh == 1)),
                 reads=[R(pT)] + res_in, writes=[R(ops_)])
    P.op("dve", lambda e: e.reciprocal(out=sm[:, 3, 0:HU], in_=sm[:, 2, 0:HU]), reads=[R(sm)], writes=[R(sm)])
    P.op("dve", lambda e: e.tensor_tensor(out=osb[:, 0:HU, :], in0=ops_[:, 0:HU * hd].rearrange("p (h d) -> p h d", d=hd),
                                          in1=sm[:, 3, 0:HU].unsqueeze(2).to_broadcast([128, HU, hd]), op=ALU.mult),
         reads=[R(ops_), R(sm)], writes=[R(osb)])
    P.dma("sp", o_dst, osb[:, 0:HU, :], reads=[R(osb)], writes=[("o", id(o_dst))])
    if lse_dst is not None:
        P.op("act", lambda e: e.activation(out=sm[:, 6, 0:HU], in_=sm[:, 2, 0:HU], func=AF.Ln), reads=[R(sm)], writes=[R(sm)])
        P.op("dve", lambda e: e.tensor_tensor(out=sm[:, 7, 0:HU], in0=sm[:, 6, 0:HU], in1=sm[:, 0, 0:HU], op=ALU.add),
             reads=[R(sm)], writes=[R(sm)])
        P.dma("sp", lse_dst, sm[:, 7, 0:HU], reads=[R(sm)], writes=[("lse", id(lse_dst))])


def attn_c_phase(P, qT, kT, v, kTp, vp, dist_c, sinks, cat):
    cfg = P.cfg
    NT, HG = cfg.NT, cfg.HG
    A = Att(P, 64, 64, HG, True, False, True)
    P.dma("sp", A.dist[:], dist_c.rearrange("a p s -> p a s"), writes=[R(A.dist)])
    P.dma("sp", A.sink[:], sinks.partition_broadcast(128), writes=[R(A.sink)])
    sl = alibi(cfg.CQ)
    nblk = NT // 128
    for g in range(cfg.CKV):
        for b in range(nblk):
            qb, kb, vb = A.qb[A.li % 3], A.kb[A.li % 3], A.vb[A.li % 3]
            A.li += 1
            pairs = []
            for hh in range(HG):
                h = g * HG + hh
                pairs.append((qb[:, hh, :], qT[h // 2, (h % 2) * 64:(h % 2) * 64 + 64, b * 128:(b + 1) * 128]))
            P.dmas("act", pairs, writes=[R(qb)])
            ksrc = kT[g // 2, (g % 2) * 64:(g % 2) * 64 + 64, :]
            if b > 0:
                P.dma("act", kb[:, 0, :], ksrc[:, (b - 1) * 128:(b + 1) * 128], writes=[R(kb)])
                P.dma("act", vb[:, :, :], v[(b - 1) * 128:(b + 1) * 128, g * 64:(g + 1) * 64].rearrange("(k s) d -> s k d", k=2),
                      writes=[R(vb)])
            else:
                kps = kTp[g // 2, (g % 2) * 64:(g % 2) * 64 + 64, :]
                P.dmas("act", [(kb[:, 0, 0:128], kps[:, NT - 128:NT]), (kb[:, 0, 128:256], ksrc[:, 0:128])], writes=[R(kb)])
                P.dmas("act", [(vb[:, 0, :], vp[NT - 128:NT, g * 64:(g + 1) * 64]), (vb[:, 1, :], v[0:128, g * 64:(g + 1) * 64])],
                       writes=[R(vb)])
            res_in = [R(qb), R(kb), R(vb)]
            for u0 in range(0, HG, A.HU):
                heads = list(range(u0, min(HG, u0 + A.HU)))
                h0 = g * HG + u0
                o_dst = cat[b * 128:(b + 1) * 128, h0 * 64:(h0 + len(heads)) * 64].rearrange("t (h d) -> t h d", d=64)
                banded_unit(P, A, qb, kb, vb, heads, lambda h: 0, [sl[g * HG + hh] for hh in range(HG)], b == 0,
                            A.sink[:, h0:h0 + len(heads)], o_dst, None, res_in)


def attn_a_phase(P, qT, kT, v, kTp, vp, dist_a, og, lse):
    cfg = P.cfg
    NT, HG = cfg.NT, cfg.HG
    A = Att(P, 128, 128, HG, False, True, False)
    P.dma("sp", A.dist[:], dist_a.rearrange("a p s -> p a s"), writes=[R(A.dist)])
    sl = alibi(3 * HG)
    for gi, (win, r) in enumerate(A_PAIRS):
        nd = NT // r
        nb = nd // 128
        h0 = gi * HG
        vcols = slice(h0 * 128, (h0 + HG) * 128)
        vr = v.rearrange("(n r) c -> r n c", r=r)
        vpr = vp.rearrange("(n r) c -> r n c", r=r)
        ogr = og[gi].rearrange("(n r) c -> r n c", r=r)
        lsr = lse[gi].rearrange("(n r) c -> r n c", r=r)
        slopes = [sl[h0 + hh] * r for hh in range(HG)]
        for rho in range(r):
            for n in range(nb):
                qb, kb, vb = A.qb[A.li % 3], A.kb[A.li % 3], A.vb[A.li % 3]
                A.li += 1
                pos = rho * nd + n * 128
                P.dma("act", qb[:, :, :], qT[h0:h0 + HG, :, pos:pos + 128].rearrange("h d t -> d h t"), writes=[R(qb)])
                if n > 0:
                    P.dma("act", kb[:, :, :], kT[h0:h0 + HG, :, pos - 128:pos + 128].rearrange("h d t -> d h t"), writes=[R(kb)])
                    P.dma("act", vb[:, :, :], vr[rho, (n - 1) * 128:(n + 1) * 128, vcols].rearrange("(k s) c -> s k c", k=2),
                          writes=[R(vb)])
                else:
                    pp = rho * nd + nd - 128
                    P.dmas("act", [(kb[:, :, 0:128], kTp[h0:h0 + HG, :, pp:pp + 128].rearrange("h d t -> d h t")),
                                   (kb[:, :, 128:256], kT[h0:h0 + HG, :, pos:pos + 128].rearrange("h d t -> d h t"))], writes=[R(kb)])
                    P.dmas("act", [(vb[:, 0, :], vpr[rho, nd - 128:nd, vcols]), (vb[:, 1, :], vr[rho, 0:128, vcols])], writes=[R(vb)])
                res_in = [R(qb), R(kb), R(vb)]
                for u0 in range(0, HG, A.HU):
                    heads = list(range(u0, min(HG, u0 + A.HU)))
                    o_dst = ogr[rho, n * 128:(n + 1) * 128, u0 * 128:(u0 + len(heads)) * 128].rearrange("t (h d) -> t h d", d=128)
                    l_dst = lsr[rho, n * 128:(n + 1) * 128, u0:u0 + len(heads)]
                    banded_unit(P, A, qb, kb, vb, heads, lambda h: h, slopes, n == 0, None, o_dst, l_dst, res_in)


def merge_a_phase(P, og, lse, cat):
    cfg = P.cfg
    NT, HG = cfg.NT, cfg.HG
    W = HG * 128
    ob = [[P.sb(f"mo{b}{g}", [128, HG, 128], F32) for g in range(3)] for b in range(2)]
    lb = [P.sb(f"ml{b}", [128, 8, 3, HG], F32) for b in range(2)]
    cb = [P.sb(f"mc{b}", [128, HG, 128], BF16) for b in range(2)]
    for i in range(NT // 128):
        o3 = ob[i % 2]
        l = lb[i % 2]
        c = cb[i % 2]
        rows = slice(i * 128, (i + 1) * 128)
        for g in range(3):
            P.dma("act", o3[g][:, :, :], og[g][rows, :].rearrange("t (h d) -> t h d", d=128), writes=[R(o3[g])])
            P.dma("act", l[:, 0, g, :], lse[g][rows, :], writes=[R(l, g)])
        lres = [R(l, g) for g in range(3)]
        P.op("dve", lambda e, l=l: e.tensor_tensor(out=l[:, 1, 0, :], in0=l[:, 0, 0, :], in1=l[:, 0, 1, :], op=ALU.max), reads=lres, writes=[R(l, "w")])
        P.op("dve", lambda e, l=l: e.tensor_tensor(out=l[:, 1, 0, :], in0=l[:, 1, 0, :], in1=l[:, 0, 2, :], op=ALU.max), reads=[R(l, "w")] + lres, writes=[R(l, "w")])
        P.op("dve", lambda e, l=l: e.tensor_tensor(out=l[:, 2, :, :], in0=l[:, 0, :, :], in1=l[:, 1, 0:1, :].to_broadcast([128, 3, HG]),
                                                   op=ALU.subtract), reads=[R(l, "w")] + lres, writes=[R(l, "w")])
        P.op("act", lambda e, l=l: e.activation(out=l[:, 3, :, :], in_=l[:, 2, :, :], func=AF.Exp), reads=[R(l, "w")], writes=[R(l, "w")])
        P.op("dve", lambda e, l=l: e.tensor_tensor(out=l[:, 4, 0, :], in0=l[:, 3, 0, :], in1=l[:, 3, 1, :], op=ALU.add), reads=[R(l, "w")], writes=[R(l, "w")])
        P.op("dve", lambda e, l=l: e.tensor_tensor(out=l[:, 4, 0, :], in0=l[:, 4, 0, :], in1=l[:, 3, 2, :], op=ALU.add), reads=[R(l, "w")], writes=[R(l, "w")])
        P.op("dve", lambda e, l=l: e.reciprocal(out=l[:, 4, 1, :], in_=l[:, 4, 0, :]), reads=[R(l, "w")], writes=[R(l, "w")])
        P.op("dve", lambda e, l=l: e.tensor_tensor(out=l[:, 5, :, :], in0=l[:, 3, :, :], in1=l[:, 4, 1:2, :].to_broadcast([128, 3, HG]),
                                                   op=ALU.mult), reads=[R(l, "w")], writes=[R(l, "w")])
        for g in range(3):
            P.op("dve" if g != 1 else "pool", lambda e, g=g, o3=o3, l=l: e.tensor_tensor(
                out=o3[g][:, :, :], in0=o3[g][:, :, :], in1=l[:, 5, g, :].unsqueeze(2).to_broadcast([128, HG, 128]), op=ALU.mult),
                reads=[R(o3[g]), R(l, "w")], writes=[R(o3[g])])
        P.op("dve", lambda e, o3=o3: e.tensor_tensor(out=o3[0][:, :, :], in0=o3[0][:, :, :], in1=o3[1][:, :, :], op=ALU.add),
             reads=[R(o3[0]), R(o3[1])], writes=[R(o3[0])])
        P.op("dve", lambda e, o3=o3, c=c: e.tensor_tensor(out=c[:, :, :], in0=o3[0][:, :, :], in1=o3[2][:, :, :], op=ALU.add),
             reads=[R(o3[0]), R(o3[2])], writes=[R(c)])
        P.dma("sp", cat[rows, 0:W].rearrange("t (h d) -> t h d", d=128), c[:, :, :], reads=[R(c)], writes=[("cat_a", i)])


def attn_b_phase(P, qT, kT, v, kTp, vp, bconst, pvalid, cat):
    cfg = P.cfg
    NT, HG = cfg.NT, cfg.HG
    nblk = NT // 128
    QG = 512
    ngr = NT // QG
    qh = [P.sb(f"bq{i}", [128, NT], BF16) for i in range(2)]
    kh = [P.sb(f"bk{i}", [128, 2, NT], BF16) for i in range(2)]
    vh = [P.sb(f"bv{i}", [128, 2, nblk, 128], BF16) for i in range(2)]
    cst = P.sb("bconst", [128, 256 + 8 * 512], F32)
    pv = P.sb("bpv", [128, 2], F32)
    Lm = cst[:, 0:128]
    On = cst[:, 128:256]
    e1 = [P.sb(f"be{i}", [128, 512], F32) for i in range(2)]
    spl = [P.sb(f"bs{i}", [128, 512], F32) for i in range(2)]
    t1 = [P.sb(f"bt{i}", [128, 512], F32) for i in range(2)]
    arg = [P.sb(f"ba{i}", [128, 512], F32) for i in range(2)]
    aT = [P.sb(f"baT{i}", [128, 512], BF16) for i in range(2)]
    Rr = P.sb("bR", [128, 512], F32)
    ost = [P.sb(f"bo{i}", [128, 4, 128], BF16) for i in range(2)]
    banks = [P.ps(f"bb{i}") for i in range(8)]
    P.dma("sp", cst[:], bconst[:, :], writes=[R(cst)])
    P.dma("sp", pv[:], pvalid[:, :], writes=[R(pv)])
    step = 0
    gcount = 0
    for hh in range(HG):
        h = 3 * HG + hh
        q_, k_, v_ = qh[hh % 2], kh[hh % 2], vh[hh % 2]
        P.dma("act", q_[:], qT[h, :, :], writes=[R(q_)])
        P.dmas("act", [(k_[:, 0, :], kTp[h, :, :]), (k_[:, 1, :], kT[h, :, :])], writes=[R(k_)])
        P.dmas("act", [(v_[:, 0, :, :], vp[:, h * 128:(h + 1) * 128].rearrange("(b s) d -> s b d", s=128)),
                       (v_[:, 1, :, :], v[:, h * 128:(h + 1) * 128].rearrange("(b s) d -> s b d", s=128))], writes=[R(v_)])
        for G in range(ngr):
            ob = banks[6 + gcount % 2]
            osb = ost[gcount % 2]
            gcount += 1
            P.op("dve", lambda e: e.memset(Rr[:], 0.0), writes=[R(Rr)])
            klist = [(1, j) for j in range(4 * G + 3, -1, -1)] + [(0, j) for j in range(nblk - 1, -1, -1)]
            started = [False] * 4
            for si, (own, j) in enumerate(klist):
                last = si == len(klist) - 1
                zb = banks[step % 2]
                pb = banks[2 + step % 2]
                cbk = banks[4 + step % 2]
                e_, s_, t_, a_, aT_ = e1[step % 2], spl[step % 2], t1[step % 2], arg[step % 2], aT[step % 2]
                step += 1
                dm = j - 4 * G if own else -1
                P.op("pe", lambda e, own=own, j=j, zb=zb, k_=k_, q_=q_, G=G: e.matmul(zb[:, :], lhsT=k_[:, own, j * 128:(j + 1) * 128],
                                                                  rhs=q_[:, G * QG:(G + 1) * QG], start=True, stop=True),
                     reads=[R(q_), R(k_)], writes=[R(zb)])
                P.op("act", lambda e, zb=zb, e_=e_: e.activation(out=e_[:], in_=zb[:, :], func=AF.Exp), reads=[R(zb)], writes=[R(e_)])
                P.op("act", lambda e, e_=e_, s_=s_: e.activation(out=s_[:], in_=e_[:], func=AF.Ln, bias=1.0, scale=1.0),
                     reads=[R(e_)], writes=[R(s_)])
                if own and dm >= 0:
                    mk = cst[:, 256 + dm * 512:256 + (dm + 1) * 512]
                    P.op("pool", lambda e, s_=s_, mk=mk: e.tensor_tensor(out=s_[:], in0=s_[:], in1=mk, op=ALU.mult),
                         reads=[R(s_), R(cst)], writes=[R(s_)])
                if not own:
                    P.op("pool", lambda e, s_=s_: e.tensor_scalar(out=s_[:], in0=s_[:], scalar1=pv[:, 0:1], scalar2=None, op0=ALU.mult),
                         reads=[R(s_), R(pv)], writes=[R(s_)])
                P.op("pe", lambda e, s_=s_, pb=pb: e.matmul(pb[:, :], lhsT=Lm, rhs=s_[:], start=True, stop=True),
                     reads=[R(s_), R(cst)], writes=[R(pb)])
                P.op("pe", lambda e, s_=s_, cbk=cbk: e.matmul(cbk[:, :], lhsT=On, rhs=s_[:], start=True, stop=True),
                     reads=[R(s_), R(cst)], writes=[R(cbk)])
                P.op("dve", lambda e, t_=t_, pb=pb: e.tensor_tensor(out=t_[:], in0=Rr[:], in1=pb[:, :], op=ALU.add),
                     reads=[R(Rr), R(pb)], writes=[R(t_)])
                P.op("dve", lambda e, a_=a_, t_=t_, zb=zb: e.tensor_tensor(out=a_[:], in0=zb[:, :], in1=t_[:], op=ALU.subtract),
                     reads=[R(zb), R(t_)], writes=[R(a_)])
                if not last:
                    P.op("dve", lambda e, cbk=cbk: e.tensor_tensor(out=Rr[:], in0=Rr[:], in1=cbk[:, :], op=ALU.add),
                         reads=[R(Rr), R(cbk)], writes=[R(Rr)])
                if own and dm >= 0:
                    nm = cst[:, 256 + (4 + dm) * 512:256 + (5 + dm) * 512]
                    P.op("pool", lambda e, a_=a_, nm=nm: e.tensor_tensor(out=a_[:], in0=a_[:], in1=nm, op=ALU.add),
                         reads=[R(a_), R(cst)], writes=[R(a_)])
                if not own:
                    P.op("act", lambda e, a_=a_, aT_=aT_: e.activation(out=aT_[:], in_=a_[:], func=AF.Exp, bias=pv[:, 1:2], scale=1.0),
                         reads=[R(a_), R(pv)], writes=[R(aT_)])
                else:
                    P.op("act", lambda e, a_=a_, aT_=aT_: e.activation(out=aT_[:], in_=a_[:], func=AF.Exp), reads=[R(a_)], writes=[R(aT_)])
                for i in range(4):
                    if si > 0 and own and j > 4 * G + i:
                        continue
                    P.op("pe", lambda e, i=i, own=own, j=j, aT_=aT_, st=(si == 0 and i == 0), last=last, ob=ob, v_=v_: e.matmul(
                        ob[:, i * 128:(i + 1) * 128], lhsT=aT_[:, i * 128:(i + 1) * 128], rhs=v_[:, own, j, :], start=st, stop=last, skip_group_check=True),
                        reads=[R(aT_), R(v_)], writes=[R(ob)])
                    started[i] = True
            P.evac_copy(osb[:, :, :], ob[:, :].rearrange("p (i d) -> p i d", d=128), reads=[R(ob)], writes=[R(osb)])
            c0 = (HG + hh) * 128
            P.dma("sp", cat[G * QG:(G + 1) * QG, c0:c0 + 128].rearrange("(i t) d -> t i d", t=128), osb[:, :, :], reads=[R(osb)],
                  writes=[("cat_b", hh, G)])


def make_consts(cfg, half):
    qi = np.arange(128)[:, None]
    si = np.arange(256)[None, :]
    dist = (qi - si + 128).astype(np.float32)

    def tab(maxd):
        ok = (dist >= 0) & (dist <= maxd)
        t = np.where(ok, dist, BIG).astype(np.float32)
        tf = t.copy()
        if half == 0:
            tf[:, :128] = BIG
        return np.stack([t, tf])

    s = np.arange(128)[:, None]
    t = np.arange(128)[None, :]
    Lm = (s >= t).astype(np.float32)
    On = np.ones((128, 128), np.float32)
    masks = []
    for dm in range(4):
        m = np.zeros((128, 512), np.float32)
        for i in range(4):
            if i > dm:
                m[:, i * 128:(i + 1) * 128] = 1.0
            elif i == dm:
                m[:, i * 128:(i + 1) * 128] = (s < t).astype(np.float32)
        masks.append(m)
    negm = [(m - 1.0) * BIG for m in masks]
    bconst = np.concatenate([Lm, On] + masks + negm, axis=1).astype(np.float32)
    pvalid = np.zeros((128, 2), np.float32)
    pvalid[:, 0] = 1.0 if half == 1 else 0.0
    pvalid[:, 1] = 0.0 if half == 1 else -30000.0
    return {
        "ident": np.eye(128, dtype=np.float32).astype(ml_dtypes.bfloat16),
        "dist_a": tab(128), "dist_c": tab(127), "bconst": bconst, "pvalid": pvalid,
    }


CONST_SPECS = {"ident": ([128, 128], BF16), "dist_a": ([2, 128, 256], F32), "dist_c": ([2, 128, 256], F32),
               "bconst": ([128, 256 + 8 * 512], F32), "pvalid": ([128, 2], F32)}


def act_specs(cfg):
    NT, HG, D = cfg.NT, cfg.HG, cfg.D
    sp = {
        "qTab": ([4 * HG, 128, NT], BF16), "kTab": ([4 * HG, 128, NT], BF16), "vab": ([NT, 4 * HG * 128], BF16),
        "kTab_p": ([4 * HG, 128, NT], BF16), "vab_p": ([NT, 4 * HG * 128], BF16),
        "qTc": ([cfg.CQ // 2, 128, NT], BF16), "kTc": ([4, 128, NT], BF16), "vc": ([NT, 512], BF16),
        "kTc_p": ([4, 128, NT], BF16), "vc_p": ([NT, 512], BF16),
        "cat_ab": ([NT, cfg.AB_OUT], BF16), "cat_c": ([NT, D], BF16),
        "x": ([NT, D], F32), "x_in": ([NT, D], F32), "out": ([NT, D], F32),
    }
    for g in range(3):
        sp[f"og{g}"] = ([NT, HG * 128], F32)
        sp[f"lse{g}"] = ([NT, HG], F32)
    return sp


def weight_specs(cfg, kind):
    D, FF = cfg.D, cfg.FF
    sp = {"f1g": [D], "f1wi": [D, 2 * FF], "f1wo": [FF, D], "mg": [D], "f2g": [D], "f2wi": [D, 2 * FF], "f2wo": [FF, D]}
    if kind == "ab":
        sp["mwi"] = [D, cfg.AB_IN]
        sp["mwo"] = [cfg.AB_OUT, D]
    else:
        sp["mwi"] = [D, cfg.C_IN]
        sp["mwo"] = [cfg.C_OUT, D]
        sp["sinks"] = [cfg.CQ]
    return sp


class Tens:
    def __init__(self, P, ext_in, ext_out):
        self.P, self.ext_in, self.ext_out = P, set(ext_in), set(ext_out)
        self.t = {}
        self.specs = dict(act_specs(P.cfg))
        for k, (s, d) in CONST_SPECS.items():
            self.specs[k] = (s, d)

    def add_spec(self, name, shape, dt=F32):
        self.specs[name] = (shape, dt)

    def __getitem__(self, name):
        if name not in self.t:
            shape, dt = self.specs[name]
            if name in self.ext_in:
                self.t[name] = self.P.din(name, shape, dt)
            elif name in self.ext_out:
                self.t[name] = self.P.dout(name, shape, dt)
            else:
                self.t[name] = self.P.dint(name, shape, dt)
        return self.t[name]


def qkv_lists(cfg, kind, Tn):
    HG = cfg.HG
    fm, vs = [], []
    if kind == "ab":
        qT, kT, v = Tn["qTab"], Tn["kTab"], Tn["vab"]
        A_HW, B_HW = cfg.A_HW, cfg.B_HW
        sc = 128 ** -0.5
        for h in range(3 * HG):
            r = A_PAIRS[h // HG][1]
            fm.append((h * 128, qT[h], r, sc))
        for h in range(3 * HG):
            r = A_PAIRS[h // HG][1]
            fm.append((A_HW + h * 128, kT[h], r, 1.0))
        for h in range(HG):
            fm.append((3 * A_HW + h * 128, qT[3 * HG + h], 1, sc))
        for h in range(HG):
            fm.append((3 * A_HW + B_HW + h * 128, kT[3 * HG + h], 1, 1.0))
        vs.append((2 * A_HW, A_HW, v[:, 0:A_HW]))
        vs.append((3 * A_HW + 2 * B_HW, B_HW, v[:, A_HW:A_HW + B_HW]))
    else:
        qT, kT, v = Tn["qTc"], Tn["kTc"], Tn["vc"]
        D = cfg.D
        for j in range(cfg.CQ // 2):
            fm.append((j * 128, qT[j], 1, 0.125))
        for j in range(4):
            fm.append((D + j * 128, kT[j], 1, 1.0))
        vs.append((D + 512, 512, v[:, :]))
    return fm, vs


def copy_x(P, src, dst):
    cfg = P.cfg
    for tt in range(cfg.NTT):
        for i in range(cfg.T // 128):
            r0 = tt * cfg.T + i * 128
            P.dma("sp", dst[r0:r0 + 128, :], src[r0:r0 + 128, :], writes=xkeys(cfg, tt, i))


def lin_phases(P, L, Tn, W, prev_kind, next_kind, Wn, xsrc=None):
    x = Tn["x"]
    if xsrc is not None:
        copy_x(P, xsrc, x)
    if prev_kind is not None:
        cat = Tn["cat_ab"] if prev_kind == "ab" else Tn["cat_c"]
        C = P.cfg.AB_OUT if prev_kind == "ab" else P.cfg.D
        outproj_phase(P, L, x, cat, C, W["mwo"])
        ffn_phase(P, L, x, W["f2g"], W["f2wi"], W["f2wo"])
    if next_kind is not None:
        ffn_phase(P, L, x, Wn["f1g"], Wn["f1wi"], Wn["f1wo"])
        fm, vs = qkv_lists(P.cfg, next_kind, Tn)
        qkv_phase(P, L, x, Wn["mg"], Wn["mwi"], fm, vs)
    else:
        final_norm_phase(P, L, x, Wn["fng"], Tn["out"])


def attn_phases(P, Tn, kind, sinks=None):
    import contextlib
    if kind == "ab":
        with contextlib.ExitStack() as st:
            P.stack = st
            attn_a_phase(P, Tn["qTab"], Tn["kTab"], Tn["vab"], Tn["kTab_p"], Tn["vab_p"], Tn["dist_a"],
                         [Tn[f"og{g}"] for g in range(3)], [Tn[f"lse{g}"] for g in range(3)])
            P.s.barrier()
        with contextlib.ExitStack() as st:
            P.stack = st
            merge_a_phase(P, [Tn[f"og{g}"] for g in range(3)], [Tn[f"lse{g}"] for g in range(3)], Tn["cat_ab"])
            P.s.barrier()
        with contextlib.ExitStack() as st:
            P.stack = st
            attn_b_phase(P, Tn["qTab"], Tn["kTab"], Tn["vab"], Tn["kTab_p"], Tn["vab_p"], Tn["bconst"], Tn["pvalid"], Tn["cat_ab"])
            P.s.barrier()
    else:
        with contextlib.ExitStack() as st:
            P.stack = st
            attn_c_phase(P, Tn["qTc"], Tn["kTc"], Tn["vc"], Tn["kTc_p"], Tn["vc_p"], Tn["dist_c"], sinks, Tn["cat_c"])
            P.s.barrier()


def declare_weights(P, kind, prefix=""):
    W = {}
    keep = ("mwo", "f2g", "f2wi", "f2wo") if prefix == "p_" else ("f1g", "f1wi", "f1wo", "mg", "mwi")
    for n, shape in weight_specs(P.cfg, kind).items():
        if prefix in ("p_", "n_") and n not in keep:
            continue
        W[n] = P.din(prefix + n, shape, F32)
    return W


_prog_cache = {}
DBG = {}


def build_lin_seg(cfg, prev_kind, next_kind):
    import contextlib
    P = Prog(cfg)
    ext_in = {"x_in", "ident"}
    ext_out = set()
    if prev_kind:
        ext_in.add("cat_ab" if prev_kind == "ab" else "cat_c")
    if next_kind == "ab":
        ext_out |= {"qTab", "kTab", "vab", "x"}
    elif next_kind == "c":
        ext_out |= {"qTc", "kTc", "vc", "x"}
    else:
        ext_out |= {"out"}
    Tn = Tens(P, ext_in, ext_out)
    Tn["ident"]
    W = declare_weights(P, prev_kind, "p_") if prev_kind else None
    if next_kind:
        Wn = declare_weights(P, next_kind, "n_")
    else:
        Wn = {"fng": P.din("fng", [cfg.D], F32)}
    with contextlib.ExitStack() as st:
        P.stack = st
        L = Lin(P)
        lin_phases(P, L, Tn, W, prev_kind, next_kind, Wn, xsrc=Tn["x_in"])
        P.s.barrier()
    P.s.emit()
    return P


def build_attn_seg(cfg, kind):
    P = Prog(cfg)
    if kind == "ab":
        ext_in = {"qTab", "kTab", "vab", "kTab_p", "vab_p", "ident", "dist_a", "bconst", "pvalid"}
        ext_out = {"cat_ab"}
        if DBG.get("dump"):
            ext_out |= {f"og{g}" for g in range(3)} | {f"lse{g}" for g in range(3)}
    else:
        ext_in = {"qTc", "kTc", "vc", "kTc_p", "vc_p", "ident", "dist_c"}
        ext_out = {"cat_c"}
    Tn = Tens(P, ext_in, ext_out)
    Tn["ident"]
    sinks = P.din("sinks", [cfg.CQ], F32) if kind == "c" else None
    attn_phases(P, Tn, kind, sinks)
    P.s.emit()
    return P


def get_prog(key, fn):
    if key not in _prog_cache:
        _prog_cache[key] = fn()
    return _prog_cache[key]


def layer_weights(inp, l):
    d = {"f1g": inp["ffn1_norm"][l], "f1wi": inp["ffn1_w_in"][l], "f1wo": inp["ffn1_w_out"][l], "mg": inp["mix_norm"][l],
         "f2g": inp["ffn2_norm"][l], "f2wi": inp["ffn2_w_in"][l], "f2wo": inp["ffn2_w_out"][l]}
    if l % 2 == 0:
        d["mwi"] = inp["ab_w_in"][l // 2]
        d["mwo"] = inp["ab_w_out"][l // 2]
    else:
        d["mwi"] = inp["c_w_in"][l // 2]
        d["mwo"] = inp["c_w_out"][l // 2]
        d["sinks"] = inp["c_sinks"][l // 2]
    return d


def run_unfused(cfg, inp):
    NC = 2 * cfg.B
    depth = cfg.depth
    x = np.ascontiguousarray(inp["x"], dtype=np.float32)
    xs = [np.ascontiguousarray(x[c // 2, (c % 2) * cfg.NT:(c % 2 + 1) * cfg.NT, :]) for c in range(NC)]
    consts = [make_consts(cfg, c % 2) for c in range(NC)]
    kinds = ["ab" if l % 2 == 0 else "c" for l in range(depth)]
    cats = None
    for l in range(depth + 1):
        prev_kind = kinds[l - 1] if l > 0 else None
        next_kind = kinds[l] if l < depth else None
        P = get_prog(("lin", prev_kind, next_kind), lambda: build_lin_seg(cfg, prev_kind, next_kind))
        maps = []
        for c in range(NC):
            m = {"x_in": xs[c], "ident": consts[c]["ident"]}
            if prev_kind:
                m["cat_ab" if prev_kind == "ab" else "cat_c"] = cats[c]
                for k, v in layer_weights(inp, l - 1).items():
                    if k in ("mwo", "f2g", "f2wi", "f2wo"):
                        m["p_" + k] = np.ascontiguousarray(v, dtype=np.float32)
            if next_kind:
                for k, v in layer_weights(inp, l).items():
                    if k in ("f1g", "f1wi", "f1wo", "mg", "mwi"):
                        m["n_" + k] = np.ascontiguousarray(v, dtype=np.float32)
            else:
                m["fng"] = np.ascontiguousarray(inp["final_norm"], dtype=np.float32)
            maps.append(m)
        res = run_bass_kernel_spmd(P.nc, maps, core_ids=list(range(NC))).results
        if next_kind is None:
            outs = [r["out"] for r in res]
            break
        xs = [r["x"] for r in res]
        DBG[("lin", l)] = res
        P2 = get_prog(("attn", next_kind), lambda: build_attn_seg(cfg, next_kind))
        maps = []
        sfx = "ab" if next_kind == "ab" else "c"
        for c in range(NC):
            pc = (c // 2) * 2
            m = {"qT" + sfx: res[c]["qT" + sfx], "kT" + sfx: res[c]["kT" + sfx], "v" + sfx: res[c]["v" + sfx],
                 "kT" + sfx + "_p": res[pc]["kT" + sfx], "v" + sfx + "_p": res[pc]["v" + sfx], "ident": consts[c]["ident"]}
            if next_kind == "ab":
                m.update({"dist_a": consts[c]["dist_a"], "bconst": consts[c]["bconst"], "pvalid": consts[c]["pvalid"]})
            else:
                m.update({"dist_c": consts[c]["dist_c"], "sinks": np.ascontiguousarray(inp["c_sinks"][l // 2], dtype=np.float32)})
            maps.append(m)
        res2 = run_bass_kernel_spmd(P2.nc, maps, core_ids=list(range(NC))).results
        cats = [r["cat_" + sfx] for r in res2]
        DBG[("attn", l)] = res2
    out = np.zeros((cfg.B, cfg.S, cfg.D), np.float32)
    for c in range(NC):
        out[c // 2, (c % 2) * cfg.NT:(c % 2 + 1) * cfg.NT, :] = outs[c]
    return out


def kernel(**inputs):
    cfg = Cfg()
    return run_unfused(cfg, inputs)
```
